# Optimizing a Trainium2 kernel written in Bass

```python
import math
import jax, jax.numpy as jnp
from jax import lax
import numpy as np

D_MODEL = 1024
BATCH = 4
SEQ = 4096
DEPTH = 4

CHUNK = 64
N_A_LAYERS = DEPTH // 2
N_B_LAYERS = DEPTH - N_A_LAYERS
EXPAND = 2
D_INNER = EXPAND * D_MODEL
GMLP_BLOCK = 128
GMLP_GROUPS = 8
GMLP_GROUP_DIM = D_INNER // GMLP_GROUPS
SB_HEADS = 16
SB_HEAD_DIM = D_INNER // SB_HEADS
Q_BLOCK = 128
EPS = 1e-6

kernel_name = 'yoco_gmlp_stickbreaking_encoder'


def rmsnorm(x, g):
    xf = x.astype(jnp.float32)
    y = xf * lax.rsqrt(jnp.mean(xf * xf, axis=-1, keepdims=True) + EPS)
    return (y * g.astype(jnp.float32)).astype(x.dtype)


def layernorm(x, g, b):
    xf = x.astype(jnp.float32)
    mu = jnp.mean(xf, axis=-1, keepdims=True)
    xc = xf - mu
    var = jnp.mean(xc * xc, axis=-1, keepdims=True)
    y = xc * lax.rsqrt(var + EPS) * g.astype(jnp.float32) + b.astype(jnp.float32)
    return y.astype(x.dtype)


def chunk_causal_mask(n):
    c = jnp.arange(n) // CHUNK
    return c[:, None] >= c[None, :]


def gmlp_mixer(h, w_in, ln_g, ln_b, w_s, b_s, w_out):
    B, S, _ = h.shape
    u, v, z = jnp.split(h @ w_in, 3, axis=-1)
    u = jax.nn.gelu(u)
    v = layernorm(jax.nn.gelu(v), ln_g, ln_b)
    nb = S // GMLP_BLOCK
    v = v.reshape(B, nb, GMLP_BLOCK, GMLP_GROUPS, GMLP_GROUP_DIM)
    w = w_s * chunk_causal_mask(GMLP_BLOCK)[None].astype(w_s.dtype)
    s = jnp.einsum('gts,bnsgc->bntgc', w, v) + b_s.T[None, None, :, :, None]
    s = s.reshape(B, S, D_INNER)
    y = u * s * jax.nn.silu(z)
    return y @ w_out


def stick_breaking_mixer(h, k, v, w_in, w_out):
    B, S, _ = h.shape
    q, z = jnp.split(h @ w_in, 2, axis=-1)
    q = q.reshape(B, S, SB_HEADS, SB_HEAD_DIM).transpose(0, 2, 1, 3)
    scale = 1.0 / math.sqrt(SB_HEAD_DIM)
    outs = []
    for start in range(0, S, Q_BLOCK):
        end = start + Q_BLOCK
        qb = q[:, :, start:end].astype(jnp.float32)
        kb = k[:, :, :end].astype(jnp.float32)
        vb = v[:, :, :end].astype(jnp.float32)
        logits = jnp.einsum('bhtd,bhsd->bhts', qb, kb) * scale
        strict = jnp.arange(end)[None, :] < jnp.arange(start, end)[:, None]
        log_beta = jax.nn.log_sigmoid(logits)
        log_1m = jnp.where(strict, jax.nn.log_sigmoid(-logits), 0.0)
        after = lax.cumsum(log_1m, axis=3, reverse=True) - log_1m
        att = jnp.where(strict, jnp.exp(log_beta + after), 0.0)
        outs.append(jnp.einsum('bhts,bhsd->bhtd', att, vb))
    o = jnp.concatenate(outs, axis=2).astype(h.dtype)
    o = o.transpose(0, 2, 1, 3).reshape(B, S, D_INNER)
    y = o * jax.nn.silu(z)
    return y @ w_out


def setup_inputs(seed: int = 0) -> dict:
    key = jax.random.key(seed)
    ks = jax.random.split(key, 16)

    def dense(k, shape, fan_in, mult=1.0):
        return jax.random.normal(k, shape, jnp.float32) * (mult * fan_in ** -0.5)

    def gain(k, shape):
        return 1.0 + 0.05 * jax.random.normal(k, shape, jnp.float32)

    x = jax.random.normal(ks[0], (BATCH, SEQ, D_MODEL), jnp.float32)
    a_norm = gain(ks[1], (N_A_LAYERS, D_MODEL))
    a_w_in = dense(ks[2], (N_A_LAYERS, D_MODEL, 3 * D_INNER), D_MODEL)
    a_ln_g = gain(ks[3], (N_A_LAYERS, D_INNER))
    a_ln_b = 0.02 * jax.random.normal(ks[4], (N_A_LAYERS, D_INNER), jnp.float32)
    a_w_s = dense(ks[5], (N_A_LAYERS, GMLP_GROUPS, GMLP_BLOCK, GMLP_BLOCK), GMLP_BLOCK, 0.5)
    a_b_s = 1.0 + 0.1 * jax.random.normal(ks[6], (N_A_LAYERS, GMLP_GROUPS, GMLP_BLOCK), jnp.float32)
    a_w_out = dense(ks[7], (N_A_LAYERS, D_INNER, D_MODEL), D_INNER)
    kv_norm = gain(ks[8], (D_MODEL,))
    w_kv = dense(ks[9], (D_MODEL, 2 * D_INNER), D_MODEL)
    b_norm = gain(ks[10], (N_B_LAYERS, D_MODEL))
    b_w_in = dense(ks[11], (N_B_LAYERS, D_MODEL, 2 * D_INNER), D_MODEL)
    b_w_out = dense(ks[12], (N_B_LAYERS, D_INNER, D_MODEL), D_INNER)
    final_norm = gain(ks[13], (D_MODEL,))
    return {'x': x, 'a_norm': a_norm, 'a_w_in': a_w_in, 'a_ln_g': a_ln_g, 'a_ln_b': a_ln_b,
            'a_w_s': a_w_s, 'a_b_s': a_b_s, 'a_w_out': a_w_out, 'kv_norm': kv_norm, 'w_kv': w_kv,
            'b_norm': b_norm, 'b_w_in': b_w_in, 'b_w_out': b_w_out, 'final_norm': final_norm}


def reference(x, a_norm, a_w_in, a_ln_g, a_ln_b, a_w_s, a_b_s, a_w_out, kv_norm, w_kv,
              b_norm, b_w_in, b_w_out, final_norm):
    B, S, _ = x.shape
    k = v = None
    for layer in range(DEPTH):
        if layer < N_A_LAYERS:
            i = layer
            h = rmsnorm(x, a_norm[i])
            x = x + gmlp_mixer(h, a_w_in[i], a_ln_g[i], a_ln_b[i], a_w_s[i], a_b_s[i], a_w_out[i])
        else:
            if k is None:
                hk = rmsnorm(x, kv_norm)
                k, v = jnp.split(hk @ w_kv, 2, axis=-1)
                k = k.reshape(B, S, SB_HEADS, SB_HEAD_DIM).transpose(0, 2, 1, 3)
                v = v.reshape(B, S, SB_HEADS, SB_HEAD_DIM).transpose(0, 2, 1, 3)
            i = layer - N_A_LAYERS
            h = rmsnorm(x, b_norm[i])
            x = x + stick_breaking_mixer(h, k, v, b_w_in[i], b_w_out[i])
    return rmsnorm(x, final_norm)
```

```python
import numpy as np
from contextlib import ExitStack
import ml_dtypes
import concourse.bass as bass
import concourse.mybir as mybir
from concourse.bass_utils import run_bass_kernel_spmd

F32 = mybir.dt.float32
BF16 = mybir.dt.bfloat16
AF = mybir.ActivationFunctionType
ALU = mybir.AluOpType

D = 1024
E = 2048
NB = 16
NTOK = 2048
EPS = 1e-6
NEG = -30000.0
QSCALE = 1.0 / float(np.sqrt(128.0))


class _Op:
    __slots__ = ("idx", "eng", "fn", "deps", "dma", "sig", "cnt")

    def __init__(self, idx, eng, fn, deps, dma):
        self.idx = idx
        self.eng = eng
        self.fn = fn
        self.deps = deps
        self.dma = dma
        self.sig = False
        self.cnt = 0


class Sched:
    ENGS = ("sp", "act", "dve", "pool", "pe")

    def __init__(self):
        self.ops = []
        self.lastw = {}
        self.readers = {}
        self.phase = None

    def add(self, eng, fn, reads=(), writes=(), dma=None):
        idx = len(self.ops)
        if self.phase is not None and self.phase not in writes:
            reads = list(reads) + [self.phase]
        deps = {}
        for r in reads:
            w = self.lastw.get(r)
            if w is not None:
                deps[w] = True
        for w_ in writes:
            w = self.lastw.get(w_)
            if w is not None:
                deps[w] = True
            for rd in self.readers.get(w_, ()):
                if rd not in deps:
                    deps[rd] = False
        for r in reads:
            self.readers.setdefault(r, []).append(idx)
        for w_ in writes:
            self.lastw[w_] = idx
            self.readers[w_] = []
        deps.pop(idx, None)
        self.ops.append(_Op(idx, eng, fn, deps, dma))
        return idx

    def streams(self):
        return sorted({op.dma for op in self.ops if op.dma is not None})

    def emit(self, block, sems):
        ops = self.ops
        edges = {}
        for op in ops:
            lst = []
            for d, hard in op.deps.items():
                dep = ops[d]
                if dep.dma is None and op.dma is None and dep.eng == op.eng:
                    if dep.eng == "pe" or not hard:
                        continue
                lst.append(d)
                dep.sig = True
            edges[op.idx] = lst
        cnt = {}
        for op in ops:
            if op.dma is not None:
                key = ("s", op.dma)
                cnt[key] = cnt.get(key, 0) + 16
                op.cnt = cnt[key]
            elif op.sig:
                key = ("e", op.eng)
                cnt[key] = cnt.get(key, 0) + 1
                op.cnt = cnt[key]

        def run_engine(ename, eng):
            waited = {}
            for op in ops:
                if op.eng != ename:
                    continue
                need = {}
                for d in edges[op.idx]:
                    dep = ops[d]
                    key = ("s", dep.dma) if dep.dma is not None else ("e", dep.eng)
                    if dep.cnt > need.get(key, 0):
                        need[key] = dep.cnt
                for key, v in need.items():
                    if waited.get(key, 0) >= v:
                        continue
                    eng.wait_ge(sems[key], v)
                    waited[key] = v
                ins = op.fn(eng)
                if op.dma is not None:
                    ins.then_inc(sems[("s", op.dma)], 16)
                elif op.sig:
                    ins.then_inc(sems[("e", op.eng)], 1)
            for key, v in cnt.items():
                if key[0] == "s":
                    if any(o.dma == key[1] and o.eng == ename for o in ops) and waited.get(key, 0) < v:
                        eng.wait_ge(sems[key], v)

        block.sync(lambda e: run_engine("sp", e))
        block.scalar(lambda e: run_engine("act", e))
        block.vector(lambda e: run_engine("dve", e))
        block.gpsimd(lambda e: run_engine("pool", e))
        block.tensor(lambda e: run_engine("pe", e))


UBYTES = 84 * 1024
DEBUG = False
LAST = {}


class Base:
    def __init__(self, nc, es):
        self.nc = nc
        self.es = es
        self.S = Sched()
        self._bank_rr = 0
        self._slot_rr = 0
        self._gn_rr = 0
        self.wc_idx = {}
        self.uoff = 0

    def sb(self, name, shape, dt):
        return self.es.enter_context(self.nc.sbuf_tensor(name, shape, dt))

    def din(self, name, shape, dt=F32):
        return self.nc.dram_tensor(name, list(shape), dt, kind="ExternalInput").ap()

    def dout(self, name, shape, dt=F32):
        return self.nc.dram_tensor(name, list(shape), dt, kind="ExternalOutput").ap()

    def carve(self, shape, dt, parts=128):
        esz = 4 if dt == F32 else 2
        n = 1
        for s_ in shape[1:]:
            n *= s_
        nbytes = (n * esz + 31) // 32 * 32
        assert self.uoff + nbytes <= UBYTES, (self.uoff, nbytes)
        a = self.ubuf[0:parts, self.uoff // 2:(self.uoff + n * esz) // 2]
        self.uoff += nbytes
        if dt == F32:
            a = a.bitcast(F32)
        if len(shape) == 3:
            a = a.rearrange("p (a b) -> p a b", a=shape[1])
        elif len(shape) == 4:
            a = a.rearrange("p (a b c) -> p a b c", a=shape[1], b=shape[2])
        return a

    def fence(self, name):
        S = self.S
        sc = self.fsc
        S.phase = None
        prev = getattr(self, "_phase_name", None)
        wr = [("phase", name)] + ([("phase", prev)] if prev else [])
        S.add("dve", lambda e: e.memset(sc[:], 0.0), writes=wr)
        self._phase_name = name
        S.phase = ("phase", name)
        self.uoff = 0

    def common_alloc(self):
        nc, es = self.nc, self.es
        self.pbig = [es.enter_context(nc.psum_tensor("pbig%d" % i, [128, 2048], F32)) for i in range(2)]
        self.xres = self.sb("xres", [128, NB, D], F32)
        self.hT = self.sb("hT", [128, 8, 512], BF16)
        self.wslot = [self.sb("wslot%d" % i, [128, 4096], BF16) for i in range(4)]
        self.hb = [self.sb("hb%d" % i, [128, D], BF16) for i in range(2)]
        self.junk = self.sb("junk", [128, D], BF16)
        self.gn = [self.sb("gn%d" % i, [128, D], F32) for i in range(2)]
        self.ss = self.sb("ss", [128, 4], F32)
        self.rs = self.sb("rs", [128, 4], F32)
        self.ident = self.sb("ident", [128, 128], BF16)
        self.identf = self.sb("identf", [128, 128], F32)
        self.fsc = self.sb("fsc", [128, 8], F32)
        self.ubuf = self.sb("ubuf", [128, UBYTES // 2], BF16)
        S = self.S
        identf, ident = self.identf, self.ident
        S.add("pool", lambda e: e.memset(identf[:], 1.0), writes=["identf"])
        S.add("pool", lambda e: e.affine_select(out=identf[:], in_=identf[:], pattern=[[-1, 128]],
                                               compare_op=ALU.is_equal, fill=0.0, base=0, channel_multiplier=1),
              reads=["identf"], writes=["identf"])
        S.add("dve", lambda e: e.tensor_copy(out=ident[:], in_=identf[:]), reads=["identf"], writes=["ident"])

    def bank(self, i):
        return self.pbig[i // 4][:, (i % 4) * 512:(i % 4 + 1) * 512]

    def bankres(self, i):
        return ("bank", i)

    def next_bank(self, pool):
        b = pool[self._bank_rr % len(pool)]
        self._bank_rr += 1
        return b

    def load_slot(self, key, src, kdim):
        S = self.S
        i = self._slot_rr % 4
        self._slot_rr += 1
        flat = self.wslot[i][:]
        view = flat.rearrange("p (k n) -> p k n", k=kdim)
        if key not in self.wc_idx:
            idx = len(self.wc_idx)
            self.wc_idx[key] = idx
            dst = self.wc[idx]
            S.add("pool", lambda e: e.dma_start(out=view, in_=src), writes=[("w", i)], dma="w%d" % i)
            S.add("sp", lambda e: e.dma_start(out=dst, in_=flat), reads=[("w", i)], writes=[("wc", idx)], dma="wcst%d" % i)
        else:
            idx = self.wc_idx[key]
            srcc = self.wc[idx]
            S.add("pool", lambda e: e.dma_start(out=flat, in_=srcc), reads=[("wc", idx)], writes=[("w", i)], dma="w%d" % i)
        return i, view

    def load_gain(self, g_ap):
        i = self._gn_rr % 2
        self._gn_rr += 1
        t = self.gn[i]
        self.S.add("sp", lambda e: e.dma_start(out=t[:], in_=g_ap.partition_broadcast(128)),
                   writes=[("gn", i)], dma="gn%d" % i)
        return t, ("gn", i)

    def rstd4(self, src, dst, n, scale, res_in, res_out):
        S = self.S
        S.add("dve", lambda e: e.tensor_scalar(out=dst[:, 0:n], in0=src[:, 0:n], scalar1=scale, scalar2=EPS,
                                              op0=ALU.mult, op1=ALU.add), reads=res_in, writes=res_out)
        S.add("act", lambda e: e.activation(out=dst[:, 0:n], in_=dst[:, 0:n], func=AF.Ln), reads=res_out, writes=res_out)
        S.add("act", lambda e: e.activation(out=dst[:, 0:n], in_=dst[:, 0:n], func=AF.Exp, scale=-0.5),
              reads=res_out, writes=res_out)

    def sumsq4(self, xv, xn):
        S = self.S
        ss, junk = self.ss, self.junk
        ssres = [("ss", j) for j in range(4)]
        S.add("dve", lambda e: e.memset(ss[:], 0.0), writes=ssres)
        for j in range(4):
            S.add("act", lambda e, j=j: e.activation(out=junk[:], in_=xv[:, j, :], func=AF.Square, accum_out=ss[:, j:j + 1]),
                  reads=[xn[j], ("ss", j)], writes=["junk", ("ss", j)])
        self.rstd4(ss, self.rs, 4, 1.0 / D, ssres, ["rs"])

    def norm_group(self, xv, xn, gt, gres, banks, copy_eng):
        S = self.S
        rs, hT, ident = self.rs, self.hT, self.ident
        self.sumsq4(xv, xn)
        for j in range(4):
            hb = self.hb[j % 2]
            hbr = ("hb", j % 2)
            S.add("dve", lambda e, j=j, hb=hb: e.scalar_tensor_tensor(
                out=hb[:], in0=xv[:, j, :], scalar=rs[:, j:j + 1], in1=gt[:], op0=ALU.mult, op1=ALU.mult),
                reads=[xn[j], "rs", gres], writes=[hbr])
            b = self.next_bank(banks)
            pv = self.bank(b).bitcast(BF16).rearrange("p (k n) -> p k n", k=8)
            for k in range(8):
                S.add("pe", lambda e, k=k, hb=hb, pv=pv: e.transpose(out=pv[:, k, :], in_=hb[:, k * 128:(k + 1) * 128],
                                                                     identity=ident[:]),
                      reads=[hbr, "ident"], writes=[self.bankres(b)])
            dst = hT[:, :, j * 128:(j + 1) * 128]
            if copy_eng == "act":
                S.add("act", lambda e, pv=pv, dst=dst: e.copy(out=dst, in_=pv), reads=[self.bankres(b)], writes=[("hT", j)])
            else:
                S.add("dve", lambda e, pv=pv, dst=dst: e.tensor_copy(out=dst, in_=pv), reads=[self.bankres(b)],
                      writes=[("hT", j)])

    def finish(self):
        nc, es, S = self.nc, self.es, self.S
        sems = {}
        for en in Sched.ENGS:
            sems[("e", en)] = es.enter_context(nc.semaphore("sem_" + en))
        for st in S.streams():
            sems[("s", st)] = es.enter_context(nc.semaphore("dsem_" + st))
        with nc.Block() as block:
            S.emit(block, sems)


HT_ALL = [("hT", j) for j in range(4)]


def build_fused():
    nc = bass.Bass("TRN2", target_bir_lowering=False)
    with ExitStack() as es:
        B = Base(nc, es)
        S = B.S
        x_own = B.din("x_own", [NTOK, D])
        x_par = B.din("x_par", [NTOK, D])
        mk_i = B.din("mk", [128, 2, 128], BF16)
        a_norm = B.din("a_norm", [2, D])
        a_w_in = B.din("a_w_in", [2, D, 3 * E])
        a_ln_g = B.din("a_ln_g", [2, E])
        a_ln_b = B.din("a_ln_b", [2, E])
        a_w_s = B.din("a_w_s", [2, 8, 128, 128])
        a_b_s = B.din("a_b_s", [2, 8, 128])
        a_w_out = B.din("a_w_out", [2, E, D])
        kv_norm = B.din("kv_norm", [D])
        w_kv = B.din("w_kv", [D, 2 * E])
        b_norm = B.din("b_norm", [2, D])
        b_w_in = B.din("b_w_in", [2, D, 2 * E])
        b_w_out = B.din("b_w_out", [2, E, D])
        final_norm = B.din("final_norm", [D])
        y_o = B.dout("y", [NTOK, D])
        if DEBUG:
            kT_d = B.dout("kT_d", [2, 16, 128, NTOK], BF16)
            v_d = B.dout("v_d", [2, NTOK, E], BF16)
        else:
            kT_d = nc.dram_tensor("kT_d", [2, 16, 128, NTOK], BF16).ap()
            v_d = nc.dram_tensor("v_d", [2, NTOK, E], BF16).ap()

        B.wc = nc.dram_tensor("wc", [64, 128, 4096], BF16).ap()
        B.common_alloc()
        xres, hT, ident = B.xres, B.hT, B.ident
        SMALL = [0, 1, 2, 3]
        SQ = [4, 5, 6, 7]

        S.add("sp", lambda e: e.dma_start(out=xres[:], in_=x_own.rearrange("(t p) d -> p t d", p=128)),
              writes=[("x", t) for t in range(NB)], dma="xld")

        B.fence("setup")
        wsT = B.carve([128, 2, 8, 128], BF16)
        bsr = B.carve([33, 2, 1024], BF16, parts=33)
        ones33 = B.carve([33, 128], BF16, parts=33)
        keep = B.uoff
        wsf = B.carve([128, 8, 128], F32)
        wsb = B.carve([128, 8, 128], BF16)
        bsf = B.carve([33, 1024], F32, parts=33)
        bshi = B.carve([33, 1024], BF16, parts=33)
        bsd = B.carve([33, 1024], F32, parts=33)
        S.add("dve", lambda e: e.memset(ones33, 1.0), writes=["ones33"])
        for l in range(2):
            S.add("sp", lambda e, l=l: e.dma_start(out=wsf, in_=a_w_s[l].rearrange("g t s -> t g s")), writes=["wsf"], dma="wsf")
            S.add("dve", lambda e: e.memset(wsf[0:64, :, 64:128], 0.0), reads=["wsf"], writes=["wsf"])
            S.add("dve", lambda e: e.tensor_copy(out=wsb, in_=wsf), reads=["wsf"], writes=["wsb"])
            b = B.next_bank(SMALL)
            pv = B.bank(b).bitcast(BF16).rearrange("p (k n) -> p k n", k=8)
            for g in range(8):
                S.add("pe", lambda e, g=g, pv=pv: e.transpose(out=pv[:, g, :], in_=wsb[:, g, :], identity=ident[:]),
                      reads=["wsb", "ident"], writes=[B.bankres(b)])
            S.add("dve", lambda e, pv=pv, l=l: e.tensor_copy(out=wsT[:, l, :, :], in_=pv), reads=[B.bankres(b)], writes=["wsT"])
            S.add("dve", lambda e: e.memset(bsf, 0.0), writes=["bsf"])
            bsrc = a_b_s[l:l + 1].rearrange("o g t -> o (g t)")
            S.add("sp", lambda e, bsrc=bsrc: e.dma_start(out=bsf[0:1, :], in_=bsrc), reads=["bsf"], writes=["bsf"], dma="bsf")
            S.add("sp", lambda e, bsrc=bsrc: e.dma_start(out=bsf[32:33, :], in_=bsrc), reads=["bsf"], writes=["bsf"], dma="bsf")
            S.add("dve", lambda e: e.tensor_copy(out=bshi, in_=bsf), reads=["bsf"], writes=["bshi"])
            S.add("dve", lambda e: e.tensor_tensor(out=bsd, in0=bsf, in1=bshi, op=ALU.subtract), reads=["bsf", "bshi"], writes=["bsd"])
            S.add("dve", lambda e, l=l: e.tensor_copy(out=bsr[:, l, :], in_=bshi), reads=["bshi"], writes=["bsr"])
            S.add("dve", lambda e, l=l: e.tensor_copy(out=bsr[32:33, l, :], in_=bsd[32:33, :]), reads=["bsd", "bsr"], writes=["bsr"])

        B.fence("A")
        B.uoff = keep
        xpar = B.carve([128, 4, D], F32)
        vtok = B.carve([128, 2, E], F32)
        vln = B.carve([128, E], BF16)
        guz = B.carve([128, 16, 512], BF16)
        szt = [B.carve([128, 512], BF16) for _ in range(2)]
        lng = B.carve([128, E], F32)
        lnb = B.carve([128, E], F32)
        bst = B.carve([128, 2, 4, 6], F32)
        mv = B.carve([128, 2, 2], F32)
        lrs = B.carve([128, 4], F32)
        var4 = B.carve([128, 4], F32)
        vst = vtok.rearrange("p a b -> p (a b)").bitcast(BF16)

        for srcx in range(2):
            for tg in range(4):
                if srcx == 0:
                    xv = xres[:, 4 * tg:4 * tg + 4, :]
                    xn = [("x", 4 * tg + j) for j in range(4)]
                else:
                    xv = xpar
                    xn = [("xp", j) for j in range(4)]
                    S.add("sp", lambda e, tg=tg: e.dma_start(
                        out=xpar, in_=x_par[tg * 512:(tg + 1) * 512, :].rearrange("(t p) d -> p t d", p=128)),
                        writes=xn, dma="xpld")
                for l in range(2):
                    gt, gres = B.load_gain(a_norm[l])
                    S.add("sp", lambda e, l=l: e.dma_start(out=lng, in_=a_ln_g[l].partition_broadcast(128)), writes=["lng"], dma="lng")
                    S.add("sp", lambda e, l=l: e.dma_start(out=lnb, in_=a_ln_b[l].partition_broadcast(128)), writes=["lnb"], dma="lnb")
                    B.norm_group(xv, xn, gt, gres, SMALL, "act")
                    for part, col0 in (("u", 0), ("z", 2 * E)):
                        for cg in range(4):
                            c0 = col0 + cg * 512
                            si, wv = B.load_slot(("a", l, part, cg), a_w_in[l][:, c0:c0 + 512].rearrange("(k p) n -> p k n", p=128), 8)
                            for cc in range(4):
                                c = 4 * cg + cc
                                b = B.next_bank(SMALL)
                                pb = B.bank(b)
                                for k in range(8):
                                    S.add("pe", lambda e, k=k, wv=wv, cc=cc, pb=pb: e.matmul(
                                        pb, lhsT=wv[:, k, cc * 128:(cc + 1) * 128], rhs=hT[:, k, :], start=(k == 0), stop=(k == 7)),
                                        reads=[("w", si)] + HT_ALL, writes=[B.bankres(b)])
                                if part == "u":
                                    S.add("act", lambda e, c=c, pb=pb: e.activation(out=guz[:, c, :], in_=pb, func=AF.Gelu_apprx_tanh),
                                          reads=[B.bankres(b)], writes=[("guz", c)])
                                else:
                                    sz = szt[c % 2]
                                    S.add("act", lambda e, sz=sz, pb=pb: e.activation(out=sz, in_=pb, func=AF.Silu),
                                          reads=[B.bankres(b)], writes=[("szt", c % 2)])
                                    S.add("dve", lambda e, c=c, sz=sz: e.tensor_tensor(out=guz[:, c, :], in0=guz[:, c, :], in1=sz,
                                                                                       op=ALU.mult),
                                          reads=[("guz", c), ("szt", c % 2)], writes=[("guz", c)])
                    for hf in range(2):
                        for cg in range(4):
                            c0 = E + cg * 512
                            si, wv = B.load_slot(("a", l, "v", cg), a_w_in[l][:, c0:c0 + 512].rearrange("(k p) n -> p k n", p=128), 8)
                            for jj in range(2):
                                j = 2 * hf + jj
                                b = B.next_bank(SMALL)
                                pb = B.bank(b)
                                for k in range(8):
                                    S.add("pe", lambda e, k=k, wv=wv, j=j, pb=pb: e.matmul(
                                        pb, lhsT=hT[:, k, j * 128:(j + 1) * 128], rhs=wv[:, k, :], start=(k == 0), stop=(k == 7)),
                                        reads=[("w", si), ("hT", j)], writes=[B.bankres(b)])
                                S.add("act", lambda e, jj=jj, cg=cg, pb=pb: e.activation(
                                    out=vtok[:, jj, cg * 512:(cg + 1) * 512], in_=pb, func=AF.Gelu_apprx_tanh),
                                    reads=[B.bankres(b)], writes=[("vtok", jj, cg)])
                        for jj in range(2):
                            for cg in range(4):
                                S.add("dve", lambda e, jj=jj, cg=cg: e.bn_stats(out=bst[:, jj, cg, :], in_=vtok[:, jj, cg * 512:(cg + 1) * 512]),
                                      reads=[("vtok", jj, cg)], writes=[("bst", jj)])
                            S.add("dve", lambda e, jj=jj: e.bn_aggr(out=mv[:, jj, :], in_=bst[:, jj, :, :].rearrange("p a b -> p (a b)")),
                                  reads=[("bst", jj)], writes=[("mv", jj)])
                            S.add("dve", lambda e, jj=jj: e.tensor_copy(out=var4[:, jj:jj + 1], in_=mv[:, jj, 1:2]), reads=[("mv", jj)],
                                  writes=[("var4", jj)])
                        B.rstd4(var4, lrs, 2, 1.0, [("var4", jj) for jj in range(2)], ["lrs"])
                        for jj in range(2):
                            j = 2 * hf + jj
                            vres = [("vtok", jj, cg) for cg in range(4)]
                            S.add("dve", lambda e, jj=jj: e.scalar_tensor_tensor(out=vtok[:, jj, :], in0=vtok[:, jj, :], scalar=mv[:, jj, 0:1],
                                                                               in1=lng, op0=ALU.subtract, op1=ALU.mult),
                                  reads=vres + [("mv", jj), "lng"], writes=vres)
                            S.add("dve", lambda e, jj=jj: e.scalar_tensor_tensor(out=vln, in0=vtok[:, jj, :], scalar=lrs[:, jj:jj + 1],
                                                                               in1=lnb, op0=ALU.mult, op1=ALU.add),
                                  reads=vres + ["lrs", "lnb"], writes=["vln"])
                            for qq in range(4):
                                b = B.next_bank(SQ)
                                pb = B.bank(b)
                                for cc in range(4):
                                    c = 4 * qq + cc
                                    g = c // 2
                                    S.add("pe", lambda e, c=c, cc=cc, g=g, pb=pb, l=l: e.matmul(
                                        pb[:, cc * 128:(cc + 1) * 128], lhsT=vln[:, c * 128:(c + 1) * 128], rhs=wsT[:, l, g, :],
                                        start=True, stop=False), reads=["vln", "wsT"], writes=[B.bankres(b)])
                                    S.add("pe", lambda e, cc=cc, g=g, pb=pb, l=l: e.matmul(
                                        pb[:, cc * 128:(cc + 1) * 128], lhsT=ones33, rhs=bsr[:, l, g * 128:(g + 1) * 128],
                                        start=False, stop=True), reads=["ones33", "bsr"], writes=[B.bankres(b)])
                                gview = guz[:, 4 * qq:4 * qq + 4, j * 128:(j + 1) * 128]
                                S.add("dve", lambda e, pb=pb, gview=gview: e.tensor_tensor(
                                    out=gview, in0=pb.rearrange("p (a b) -> p a b", a=4), in1=gview, op=ALU.mult),
                                    reads=[B.bankres(b)] + [("guz", 4 * qq + cc) for cc in range(4)],
                                    writes=[("guz", 4 * qq + cc) for cc in range(4)])
                    wo = []
                    for q in range(4):
                        wo.append(B.load_slot(("a", l, "wo", q), a_w_out[l][q * 512:(q + 1) * 512, :].rearrange("(k p) n -> p k n", p=128), 4))
                    for j in range(4):
                        for half in range(2):
                            b = B.next_bank(SMALL)
                            pb = B.bank(b)
                            for c in range(16):
                                si, wv = wo[c // 4]
                                S.add("pe", lambda e, c=c, wv=wv, half=half, pb=pb, j=j: e.matmul(
                                    pb, lhsT=guz[:, c, j * 128:(j + 1) * 128], rhs=wv[:, c % 4, half * 512:(half + 1) * 512],
                                    start=(c == 0), stop=(c == 15)), reads=[("guz", c), ("w", si)], writes=[B.bankres(b)])
                            xw = xv[:, j, half * 512:(half + 1) * 512]
                            S.add("dve", lambda e, xw=xw, pb=pb: e.tensor_tensor(out=xw, in0=xw, in1=pb, op=ALU.add),
                                  reads=[xn[j], B.bankres(b)], writes=[xn[j]])
                gt, gres = B.load_gain(kv_norm)
                B.norm_group(xv, xn, gt, gres, SMALL, "act")
                for cg in range(4):
                    si, wv = B.load_slot(("kv", "k", cg), w_kv[:, cg * 512:(cg + 1) * 512].rearrange("(k p) n -> p k n", p=128), 8)
                    kres = [("guz", 4 * (cg % 2) + hh) for hh in range(4)]
                    for hh in range(4):
                        b = B.next_bank(SMALL)
                        pb = B.bank(b)
                        for k in range(8):
                            S.add("pe", lambda e, k=k, wv=wv, hh=hh, pb=pb: e.matmul(
                                pb, lhsT=wv[:, k, hh * 128:(hh + 1) * 128], rhs=hT[:, k, :], start=(k == 0), stop=(k == 7)),
                                reads=[("w", si)] + HT_ALL, writes=[B.bankres(b)])
                        dst = guz[:, 4 * (cg % 2) + hh, :]
                        S.add("act", lambda e, dst=dst, pb=pb: e.copy(out=dst, in_=pb), reads=[B.bankres(b)], writes=[kres[hh]])
                    src = guz[:, 4 * (cg % 2):4 * (cg % 2) + 4, :]
                    dstd = kT_d[srcx, 4 * cg:4 * cg + 4, :, tg * 512:(tg + 1) * 512].rearrange("h d t -> d h t")
                    S.add("sp", lambda e, src=src, dstd=dstd: e.dma_start(out=dstd, in_=src), reads=kres, dma="kst%d" % (cg % 2))
                for cg in range(4):
                    si, wv = B.load_slot(("kv", "v", cg), w_kv[:, E + cg * 512:E + (cg + 1) * 512].rearrange("(k p) n -> p k n", p=128), 8)
                    for j in range(4):
                        b = B.next_bank(SMALL)
                        pb = B.bank(b)
                        for k in range(8):
                            S.add("pe", lambda e, k=k, wv=wv, j=j, pb=pb: e.matmul(
                                pb, lhsT=hT[:, k, j * 128:(j + 1) * 128], rhs=wv[:, k, :], start=(k == 0), stop=(k == 7)),
                                reads=[("w", si), ("hT", j)], writes=[B.bankres(b)])
                        dst = vst[:, j * 2048 + cg * 512: j * 2048 + (cg + 1) * 512]
                        S.add("dve", lambda e, dst=dst, pb=pb: e.tensor_copy(out=dst, in_=pb), reads=[B.bankres(b)],
                              writes=[("vtok", j // 2, 2 * (j % 2) + cg // 2)])
                for j in range(4):
                    t = 4 * tg + j
                    src = vst[:, j * 2048:(j + 1) * 2048]
                    S.add("sp", lambda e, src=src, t=t, srcx=srcx: e.dma_start(out=v_d[srcx, t * 128:(t + 1) * 128, :], in_=src),
                          reads=[("vtok", j // 2, 2 * (j % 2) + h2) for h2 in range(2)], dma="vst%d" % j)

        B.fence("B")
        qT = B.carve([128, 16, 512], BF16)
        yT = B.carve([128, 16, 512], BF16)
        KT = [B.carve([128, 4096], BF16) for _ in range(2)]
        VT = [B.carve([128, 32, 128], BF16) for _ in range(2)]
        Eb = [B.carve([128, 512], F32) for _ in range(2)]
        SPb = [B.carve([128, 512], BF16) for _ in range(2)]
        ATb = [B.carve([128, 512], BF16) for _ in range(2)]
        Rhl = B.carve([1, 6, 512], BF16, parts=1)
        Rhi = [Rhl[:, i, :] for i in range(3)]
        Rlo = [Rhl[:, 3 + i, :] for i in range(3)]
        mk = B.carve([128, 2, 128], BF16)
        ntri = B.carve([128, 128], BF16)
        ntrif = B.carve([128, 128], F32)
        ones1 = B.carve([128, 1], BF16)
        nones = B.carve([1, 128], BF16, parts=1)
        outb = B.carve([128, D], F32)

        S.add("pool", lambda e: e.memset(ntrif, -1.0), writes=["ntrif"])
        S.add("pool", lambda e: e.affine_select(out=ntrif, in_=ntrif, pattern=[[-1, 128]], compare_op=ALU.is_ge,
                                               fill=0.0, base=0, channel_multiplier=1), reads=["ntrif"], writes=["ntrif"])
        S.add("dve", lambda e: e.tensor_copy(out=ntri, in_=ntrif), reads=["ntrif"], writes=["ntri"])
        S.add("dve", lambda e: e.memset(ones1, 1.0), writes=["ones1"])
        S.add("dve", lambda e: e.memset(nones, -1.0), writes=["nones"])
        S.add("sp", lambda e: e.dma_start(out=mk, in_=mk_i), writes=["mk"], dma="mk")

        ABANK = [0, 1, 2]
        OBANK = [3, 4]
        RBANK = [5, 6]
        PROJ = [3, 4, 5, 6, 7]
        head_ctr = 0

        def slot_col(k):
            return (2048 + (k // 2) * 128) if k % 2 == 0 else ((k - 1) // 2) * 128

        for l in range(2):
            for tg in range(4):
                xv = xres[:, 4 * tg:4 * tg + 4, :]
                xn = [("x", 4 * tg + j) for j in range(4)]
                gt, gres = B.load_gain(b_norm[l])
                B.norm_group(xv, xn, gt, gres, PROJ, "dve")
                nsl = 8 * tg + 8
                nloc = 4 * tg + 4
                nblk = [8 * tg + 2, 8 * tg + 4, 8 * tg + 6, 8 * tg + 8]
                for part, col0 in (("q", 0), ("z", E)):
                    for cg in range(4):
                        c0 = col0 + cg * 512
                        si, wv = B.load_slot(("b", l, part, cg), b_w_in[l][:, c0:c0 + 512].rearrange("(k p) n -> p k n", p=128), 8)
                        for cc in range(4):
                            c = 4 * cg + cc
                            b = B.next_bank(PROJ)
                            pb = B.bank(b)
                            for k in range(8):
                                S.add("pe", lambda e, k=k, wv=wv, cc=cc, pb=pb: e.matmul(
                                    pb, lhsT=wv[:, k, cc * 128:(cc + 1) * 128], rhs=hT[:, k, :], start=(k == 0), stop=(k == 7)),
                                    reads=[("w", si)] + HT_ALL, writes=[B.bankres(b)])
                            if part == "q":
                                S.add("dve", lambda e, c=c, pb=pb: e.tensor_scalar(out=qT[:, c, :], in0=pb, scalar1=QSCALE,
                                                                                   scalar2=None, op0=ALU.mult),
                                      reads=[B.bankres(b)], writes=[("qT", c)])
                            else:
                                S.add("act", lambda e, c=c, pb=pb: e.activation(out=yT[:, c, :], in_=pb, func=AF.Silu),
                                      reads=[B.bankres(b)], writes=[("yT", c)])
                for hd in range(16):
                    kb = head_ctr % 2
                    head_ctr += 1
                    Kt, Vt = KT[kb], VT[kb]
                    for sx in range(2):
                        S.add("sp", lambda e, Kt=Kt, hd=hd, nloc=nloc, sx=sx: e.dma_start(
                            out=Kt[:, sx * 2048:sx * 2048 + nloc * 128], in_=kT_d[sx, hd, :, 0:nloc * 128]),
                            writes=[("KT", kb)], dma="KT%d%d" % (kb, sx))
                        S.add("sp", lambda e, Vt=Vt, hd=hd, nloc=nloc, sx=sx: e.dma_start(
                            out=Vt[:, sx * 16:sx * 16 + nloc, :],
                            in_=v_d[sx, 0:nloc * 128, hd * 128:(hd + 1) * 128].rearrange("(k p) d -> p k d", p=128)),
                            writes=[("VT", kb)], dma="VT%d%d" % (kb, sx))
                    ob, rb = OBANK[kb], RBANK[kb]
                    Ob = B.bank(ob)
                    Rb = B.bank(rb)[0:1, :]
                    S.add("dve", lambda e, Ob=Ob: e.memset(Ob, 0.0), writes=[B.bankres(ob)])
                    S.add("dve", lambda e, Rb=Rb: e.memset(Rb, 0.0), writes=[B.bankres(rb)])
                    steps = list(range(nsl - 1, -1, -1))
                    ns = len(steps)

                    def c0_of(k, nblk=nblk):
                        return 128 * sum(1 for n in nblk if n <= k)

                    def logits(s, Kt=Kt, hd=hd, kb=kb, nblk=nblk, steps=steps):
                        k = steps[s]
                        c0 = c0_of(k)
                        ab = ABANK[s % 3]
                        Ab = B.bank(ab)
                        msk = [(i, 0) for i in range(4) if nblk[i] - 1 == k]
                        if k == 0:
                            msk += [(i, 1) for i in range(4)]
                        kc = slot_col(k)
                        S.add("pe", lambda e, Ab=Ab, kc=kc, c0=c0, last=(not msk), Kt=Kt, hd=hd: e.matmul(
                            Ab[:, c0:512], lhsT=Kt[:, kc:kc + 128], rhs=qT[:, hd, c0:512], start=True, stop=last),
                            reads=[("KT", kb), ("qT", hd)], writes=[B.bankres(ab)])
                        for n_, (i, kind) in enumerate(msk):
                            S.add("pe", lambda e, Ab=Ab, i=i, kind=kind, last=(n_ == len(msk) - 1): e.matmul(
                                Ab[:, i * 128:(i + 1) * 128], lhsT=ident[:], rhs=mk[:, kind, :], start=False, stop=last),
                                reads=["ident", "mk"], writes=[B.bankres(ab)])

                    def carry(s, steps=steps):
                        k = steps[s]
                        c1 = c0_of(k + 1) if s > 0 else 512
                        if c1 >= 512:
                            return
                        ab = ABANK[s % 3]
                        Ab = B.bank(ab)
                        rv = s % 3
                        for Rt, nm in ((Rhi[rv], "Rhi"), (Rlo[rv], "Rlo")):
                            S.add("pe", lambda e, Ab=Ab, Rt=Rt, c1=c1: e.matmul(
                                Ab[:, c1:512], lhsT=nones, rhs=Rt[:, c1:512], start=False, stop=True, skip_group_check=True),
                                reads=["nones", (nm, rv)], writes=[B.bankres(ab)])

                    def stage_a(s, Rb=Rb, rb=rb, steps=steps, ns=ns):
                        k = steps[s]
                        c0 = c0_of(k)
                        ab = ABANK[s % 3]
                        Ab = B.bank(ab)
                        Et, SPt = Eb[s % 2], SPb[s % 2]
                        S.add("act", lambda e, Ab=Ab, Et=Et, c0=c0: e.activation(out=Et[:, c0:512], in_=Ab[:, c0:512], func=AF.Exp),
                              reads=[B.bankres(ab)], writes=[("E", s % 2)])
                        S.add("act", lambda e, Et=Et, SPt=SPt, c0=c0: e.activation(out=SPt[:, c0:512], in_=Et[:, c0:512], func=AF.Ln,
                                                                                    bias=1.0),
                              reads=[("E", s % 2)], writes=[("SP", s % 2)])
                        S.add("pe", lambda e, Ab=Ab, SPt=SPt, c0=c0: e.matmul(
                            Ab[:, c0:512], lhsT=ntri, rhs=SPt[:, c0:512], start=False, stop=True, skip_group_check=True),
                            reads=["ntri", ("SP", s % 2)], writes=[B.bankres(ab)])
                        if s + 1 < ns:
                            S.add("pe", lambda e, SPt=SPt, c0=c0, Rb=Rb: e.matmul(
                                Rb[:, c0:512], lhsT=ones1, rhs=SPt[:, c0:512], start=False, stop=True, skip_group_check=True),
                                reads=["ones1", ("SP", s % 2)], writes=[B.bankres(rb)])
                            rv = (s + 1) % 3
                            S.add("dve", lambda e, rv=rv, c0=c0, Rb=Rb: e.tensor_copy(out=Rhi[rv][:, c0:512], in_=Rb[:, c0:512]),
                                  reads=[B.bankres(rb)], writes=[("Rhi", rv)])
                            S.add("dve", lambda e, rv=rv, c0=c0, Rb=Rb: e.tensor_tensor(out=Rlo[rv][:, c0:512], in0=Rb[:, c0:512],
                                                                                        in1=Rhi[rv][:, c0:512], op=ALU.subtract),
                                  reads=[B.bankres(rb), ("Rhi", rv)], writes=[("Rlo", rv)])

                    def stage_b(s, Ob=Ob, ob=ob, Vt=Vt, kb=kb, steps=steps):
                        k = steps[s]
                        c0 = c0_of(k)
                        ab = ABANK[s % 3]
                        Ab = B.bank(ab)
                        At = ATb[s % 2]
                        vb = slot_col(k) // 128
                        S.add("act", lambda e, Ab=Ab, At=At, c0=c0: e.activation(out=At[:, c0:512], in_=Ab[:, c0:512], func=AF.Exp),
                              reads=[B.bankres(ab)], writes=[("AT", s % 2)])
                        S.add("pe", lambda e, At=At, vb=vb, c0=c0, Ob=Ob, Vt=Vt: e.matmul(
                            Ob[:, c0:512], lhsT=Vt[:, vb, :], rhs=At[:, c0:512], start=False, stop=True, skip_group_check=True),
                            reads=[("VT", kb), ("AT", s % 2)], writes=[B.bankres(ob)])

                    logits(0)
                    for s in range(ns + 1):
                        if s + 1 < ns:
                            logits(s + 1)
                        if s >= 1:
                            carry(s - 1)
                        if s < ns:
                            stage_a(s)
                        if s >= 1:
                            stage_b(s - 1)
                    S.add("dve", lambda e, Ob=Ob, hd=hd: e.tensor_tensor(out=yT[:, hd, :], in0=Ob, in1=yT[:, hd, :], op=ALU.mult),
                          reads=[B.bankres(ob), ("yT", hd)], writes=[("yT", hd)])
                wo = []
                for q in range(4):
                    wo.append(B.load_slot(("b", l, "wo", q), b_w_out[l][q * 512:(q + 1) * 512, :].rearrange("(k p) n -> p k n", p=128), 4))
                for j in range(4):
                    t = 4 * tg + j
                    for half in range(2):
                        b = B.next_bank(PROJ)
                        pb = B.bank(b)
                        for c in range(16):
                            si, wv = wo[c // 4]
                            S.add("pe", lambda e, c=c, wv=wv, half=half, pb=pb, j=j: e.matmul(
                                pb, lhsT=yT[:, c, j * 128:(j + 1) * 128], rhs=wv[:, c % 4, half * 512:(half + 1) * 512],
                                start=(c == 0), stop=(c == 15)), reads=[("yT", c), ("w", si)], writes=[B.bankres(b)])
                        xw = xres[:, t, half * 512:(half + 1) * 512]
                        S.add("dve", lambda e, xw=xw, pb=pb: e.tensor_tensor(out=xw, in0=xw, in1=pb, op=ALU.add),
                              reads=[("x", t), B.bankres(b)], writes=[("x", t)])

        gt, gres = B.load_gain(final_norm)
        rs = B.rs
        for tg in range(4):
            xv = xres[:, 4 * tg:4 * tg + 4, :]
            xn = [("x", 4 * tg + j) for j in range(4)]
            B.sumsq4(xv, xn)
            for j in range(4):
                t = 4 * tg + j
                S.add("dve", lambda e, t=t, j=j, gt=gt: e.scalar_tensor_tensor(
                    out=outb, in0=xres[:, t, :], scalar=rs[:, j:j + 1], in1=gt[:], op0=ALU.mult, op1=ALU.mult),
                    reads=[("x", t), "rs", gres], writes=["outb"])
                S.add("sp", lambda e, t=t: e.dma_start(out=y_o[t * 128:(t + 1) * 128, :], in_=outb),
                      reads=["outb"], dma="ost")
        B.finish()
    return nc


def _mask_tiles(j):
    s = np.arange(128)[:, None]
    t = np.arange(128)[None, :]
    tri = np.where(s >= t, NEG, 0.0).astype(np.float32)
    p0 = np.full((128, 128), NEG if j == 0 else 0.0, np.float32)
    return np.ascontiguousarray(np.stack([tri, p0], axis=1)).astype(ml_dtypes.bfloat16)


_NC_CACHE = {}


def kernel(x, a_norm, a_w_in, a_ln_g, a_ln_b, a_w_s, a_b_s, a_w_out, kv_norm, w_kv,
           b_norm, b_w_in, b_w_out, final_norm):
    f = lambda a: np.ascontiguousarray(np.asarray(a, dtype=np.float32))
    x = f(x)
    Bn, Sq, _ = x.shape
    xb = x.reshape(Bn, Sq // 128, 128, D)
    wts = {"a_norm": f(a_norm), "a_w_in": f(a_w_in), "a_ln_g": f(a_ln_g), "a_ln_b": f(a_ln_b), "a_w_s": f(a_w_s),
           "a_b_s": f(a_b_s), "a_w_out": f(a_w_out), "kv_norm": f(kv_norm), "w_kv": f(w_kv), "b_norm": f(b_norm),
           "b_w_in": f(b_w_in), "b_w_out": f(b_w_out), "final_norm": f(final_norm)}
    cores = [(b, j) for b in range(Bn) for j in range(2)]
    in_maps = []
    for (b, j) in cores:
        own = np.ascontiguousarray(xb[b, j::2].reshape(NTOK, D))
        if j == 1:
            par = np.ascontiguousarray(xb[b, 0::2].reshape(NTOK, D))
        else:
            par = np.zeros((NB, 128, D), np.float32)
            par[1:] = xb[b, 1::2][:NB - 1]
            par = par.reshape(NTOK, D)
        m = {"x_own": own, "x_par": par, "mk": _mask_tiles(j)}
        m.update(wts)
        in_maps.append(m)
    if "f" not in _NC_CACHE:
        _NC_CACHE["f"] = build_fused()
    res = run_bass_kernel_spmd(_NC_CACHE["f"], in_maps, core_ids=list(range(8))).results
    if DEBUG:
        LAST["res"] = res
    out = np.zeros((Bn, Sq // 128, 128, D), dtype=np.float32)
    for ci, (b, j) in enumerate(cores):
        out[b, j::2] = np.asarray(res[ci]["y"], dtype=np.float32).reshape(NB, 128, D)
    return out.reshape(Bn, Sq, D)
```

```python
import numpy as np
from contextlib import ExitStack
import ml_dtypes
import concourse.bass as bass
import concourse.mybir as mybir
from concourse.bass_utils import run_bass_kernel_spmd

F32 = mybir.dt.float32
BF16 = mybir.dt.bfloat16
AF = mybir.ActivationFunctionType
ALU = mybir.AluOpType

D = 1024
E = 2048
NB = 16
NTOK = 2048
EPS = 1e-6
NEG = -30000.0
QSCALE = 1.0 / float(np.sqrt(128.0))


class _Op:
    __slots__ = ("idx", "eng", "fn", "deps", "dma", "sig", "cnt")

    def __init__(self, idx, eng, fn, deps, dma):
        self.idx = idx
        self.eng = eng
        self.fn = fn
        self.deps = deps
        self.dma = dma
        self.sig = False
        self.cnt = 0


class Sched:
    ENGS = ("sp", "act", "dve", "pool", "pe")

    def __init__(self):
        self.ops = []
        self.lastw = {}
        self.readers = {}
        self.phase = None

    def add(self, eng, fn, reads=(), writes=(), dma=None):
        idx = len(self.ops)
        if self.phase is not None and self.phase not in writes:
            reads = list(reads) + [self.phase]
        deps = {}
        for r in reads:
            w = self.lastw.get(r)
            if w is not None:
                deps[w] = True
        for w_ in writes:
            w = self.lastw.get(w_)
            if w is not None:
                deps[w] = True
            for rd in self.readers.get(w_, ()):
                if rd not in deps:
                    deps[rd] = False
        for r in reads:
            self.readers.setdefault(r, []).append(idx)
        for w_ in writes:
            self.lastw[w_] = idx
            self.readers[w_] = []
        deps.pop(idx, None)
        self.ops.append(_Op(idx, eng, fn, deps, dma))
        return idx

    def streams(self):
        return sorted({op.dma for op in self.ops if op.dma is not None})

    def emit(self, block, sems):
        ops = self.ops
        edges = {}
        for op in ops:
            lst = []
            for d, hard in op.deps.items():
                dep = ops[d]
                if dep.dma is None and op.dma is None and dep.eng == op.eng:
                    if dep.eng == "pe" or not hard:
                        continue
                lst.append(d)
                dep.sig = True
            edges[op.idx] = lst
        cnt = {}
        for op in ops:
            if op.dma is not None:
                key = ("s", op.dma)
                cnt[key] = cnt.get(key, 0) + 16
                op.cnt = cnt[key]
            elif op.sig:
                key = ("e", op.eng)
                cnt[key] = cnt.get(key, 0) + 1
                op.cnt = cnt[key]

        def run_engine(ename, eng):
            waited = {}
            for op in ops:
                if op.eng != ename:
                    continue
                need = {}
                for d in edges[op.idx]:
                    dep = ops[d]
                    key = ("s", dep.dma) if dep.dma is not None else ("e", dep.eng)
                    if dep.cnt > need.get(key, 0):
                        need[key] = dep.cnt
                for key, v in need.items():
                    if waited.get(key, 0) >= v:
                        continue
                    eng.wait_ge(sems[key], v)
                    waited[key] = v
                ins = op.fn(eng)
                if op.dma is not None:
                    ins.then_inc(sems[("s", op.dma)], 16)
                elif op.sig:
                    ins.then_inc(sems[("e", op.eng)], 1)
            for key, v in cnt.items():
                if key[0] == "s":
                    if any(o.dma == key[1] and o.eng == ename for o in ops) and waited.get(key, 0) < v:
                        eng.wait_ge(sems[key], v)

        block.sync(lambda e: run_engine("sp", e))
        block.scalar(lambda e: run_engine("act", e))
        block.vector(lambda e: run_engine("dve", e))
        block.gpsimd(lambda e: run_engine("pool", e))
        block.tensor(lambda e: run_engine("pe", e))


UBYTES = 86 * 1024
DEBUG = False
LAST = {}


class Base:
    def __init__(self, nc, es):
        self.nc = nc
        self.es = es
        self.S = Sched()
        self._bank_rr = 0
        self._slot_rr = 0
        self._gn_rr = 0
        self.wc_idx = {}
        self.uoff = 0

    def sb(self, name, shape, dt):
        return self.es.enter_context(self.nc.sbuf_tensor(name, shape, dt))

    def din(self, name, shape, dt=F32):
        return self.nc.dram_tensor(name, list(shape), dt, kind="ExternalInput").ap()

    def dout(self, name, shape, dt=F32):
        return self.nc.dram_tensor(name, list(shape), dt, kind="ExternalOutput").ap()

    def carve(self, shape, dt, parts=128):
        esz = 4 if dt == F32 else 2
        n = 1
        for s_ in shape[1:]:
            n *= s_
        nbytes = (n * esz + 31) // 32 * 32
        assert self.uoff + nbytes <= UBYTES, (self.uoff, nbytes)
        a = self.ubuf[0:parts, self.uoff // 2:(self.uoff + n * esz) // 2]
        self.uoff += nbytes
        if dt == F32:
            a = a.bitcast(F32)
        if len(shape) == 3:
            a = a.rearrange("p (a b) -> p a b", a=shape[1])
        elif len(shape) == 4:
            a = a.rearrange("p (a b c) -> p a b c", a=shape[1], b=shape[2])
        return a

    def fence(self, name):
        S = self.S
        sc = self.fsc
        S.phase = None
        prev = getattr(self, "_phase_name", None)
        wr = [("phase", name)] + ([("phase", prev)] if prev else [])
        S.add("dve", lambda e: e.memset(sc[:], 0.0), writes=wr)
        self._phase_name = name
        S.phase = ("phase", name)
        self.uoff = 0

    def common_alloc(self):
        nc, es = self.nc, self.es
        self.pbig = [es.enter_context(nc.psum_tensor("pbig%d" % i, [128, 2048], F32)) for i in range(2)]
        self.xres = self.sb("xres", [128, NB, D], F32)
        self.hT = self.sb("hT", [128, 8, 512], BF16)
        self.wslot = [self.sb("wslot%d" % i, [128, 4096], BF16) for i in range(4)]
        self.hb = [self.sb("hb%d" % i, [128, D], BF16) for i in range(2)]
        self.junk = self.sb("junk", [128, D], BF16)
        self.gn = [self.sb("gn%d" % i, [128, D], F32) for i in range(2)]
        self.ss = self.sb("ss", [128, 4], F32)
        self.rs = self.sb("rs", [128, 4], F32)
        self.ident = self.sb("ident", [128, 128], BF16)
        self.identf = self.sb("identf", [128, 128], F32)
        self.fsc = self.sb("fsc", [128, 8], F32)
        self.ubuf = self.sb("ubuf", [128, UBYTES // 2], BF16)
        S = self.S
        identf, ident = self.identf, self.ident
        S.add("pool", lambda e: e.memset(identf[:], 1.0), writes=["identf"])
        S.add("pool", lambda e: e.affine_select(out=identf[:], in_=identf[:], pattern=[[-1, 128]],
                                               compare_op=ALU.is_equal, fill=0.0, base=0, channel_multiplier=1),
              reads=["identf"], writes=["identf"])
        S.add("dve", lambda e: e.tensor_copy(out=ident[:], in_=identf[:]), reads=["identf"], writes=["ident"])

    def bank(self, i):
        return self.pbig[i // 4][:, (i % 4) * 512:(i % 4 + 1) * 512]

    def bankres(self, i):
        return ("bank", i)

    def next_bank(self, pool):
        b = pool[self._bank_rr % len(pool)]
        self._bank_rr += 1
        return b

    def load_slot(self, key, src, kdim):
        S = self.S
        i = self._slot_rr % 4
        self._slot_rr += 1
        flat = self.wslot[i][:]
        view = flat.rearrange("p (k n) -> p k n", k=kdim)
        if key not in self.wc_idx:
            idx = len(self.wc_idx)
            self.wc_idx[key] = idx
            dst = self.wc[idx]
            S.add("pool", lambda e: e.dma_start(out=view, in_=src), writes=[("w", i)], dma="w%d" % i)
            S.add("sp", lambda e: e.dma_start(out=dst, in_=flat), reads=[("w", i)], writes=[("wc", idx)], dma="wcst%d" % i)
        else:
            idx = self.wc_idx[key]
            srcc = self.wc[idx]
            S.add("pool", lambda e: e.dma_start(out=flat, in_=srcc), reads=[("wc", idx)], writes=[("w", i)], dma="w%d" % i)
        return i, view

    def load_gain(self, g_ap):
        i = self._gn_rr % 2
        self._gn_rr += 1
        t = self.gn[i]
        self.S.add("sp", lambda e: e.dma_start(out=t[:], in_=g_ap.partition_broadcast(128)),
                   writes=[("gn", i)], dma="gn%d" % i)
        return t, ("gn", i)

    def rstd4(self, src, dst, n, scale, res_in, res_out):
        S = self.S
        S.add("dve", lambda e: e.tensor_scalar(out=dst[:, 0:n], in0=src[:, 0:n], scalar1=scale, scalar2=EPS,
                                              op0=ALU.mult, op1=ALU.add), reads=res_in, writes=res_out)
        S.add("act", lambda e: e.activation(out=dst[:, 0:n], in_=dst[:, 0:n], func=AF.Ln), reads=res_out, writes=res_out)
        S.add("act", lambda e: e.activation(out=dst[:, 0:n], in_=dst[:, 0:n], func=AF.Exp, scale=-0.5),
              reads=res_out, writes=res_out)

    def sumsq4(self, xv, xn):
        S = self.S
        ss, junk = self.ss, self.junk
        ssres = [("ss", j) for j in range(4)]
        S.add("dve", lambda e: e.memset(ss[:], 0.0), writes=ssres)
        for j in range(4):
            S.add("act", lambda e, j=j: e.activation(out=junk[:], in_=xv[:, j, :], func=AF.Square, accum_out=ss[:, j:j + 1]),
                  reads=[xn[j], ("ss", j)], writes=["junk", ("ss", j)])
        self.rstd4(ss, self.rs, 4, 1.0 / D, ssres, ["rs"])

    def norm_group(self, xv, xn, gt, gres, banks, copy_eng):
        S = self.S
        rs, hT, ident = self.rs, self.hT, self.ident
        self.sumsq4(xv, xn)
        for j in range(4):
            hb = self.hb[j % 2]
            hbr = ("hb", j % 2)
            S.add("dve", lambda e, j=j, hb=hb: e.scalar_tensor_tensor(
                out=hb[:], in0=xv[:, j, :], scalar=rs[:, j:j + 1], in1=gt[:], op0=ALU.mult, op1=ALU.mult),
                reads=[xn[j], "rs", gres], writes=[hbr])
            b = self.next_bank(banks)
            pv = self.bank(b).bitcast(BF16).rearrange("p (k n) -> p k n", k=8)
            for k in range(8):
                S.add("pe", lambda e, k=k, hb=hb, pv=pv: e.transpose(out=pv[:, k, :], in_=hb[:, k * 128:(k + 1) * 128],
                                                                     identity=ident[:]),
                      reads=[hbr, "ident"], writes=[self.bankres(b)])
            dst = hT[:, :, j * 128:(j + 1) * 128]
            if copy_eng == "act":
                S.add("act", lambda e, pv=pv, dst=dst: e.copy(out=dst, in_=pv), reads=[self.bankres(b)], writes=[("hT", j)])
            else:
                S.add("dve", lambda e, pv=pv, dst=dst: e.tensor_copy(out=dst, in_=pv), reads=[self.bankres(b)],
                      writes=[("hT", j)])

    def finish(self):
        nc, es, S = self.nc, self.es, self.S
        sems = {}
        for en in Sched.ENGS:
            sems[("e", en)] = es.enter_context(nc.semaphore("sem_" + en))
        for st in S.streams():
            sems[("s", st)] = es.enter_context(nc.semaphore("dsem_" + st))
        with nc.Block() as block:
            S.emit(block, sems)


HT_ALL = [("hT", j) for j in range(4)]


def build_fused():
    nc = bass.Bass("TRN2", target_bir_lowering=False)
    with ExitStack() as es:
        B = Base(nc, es)
        S = B.S
        x_own = B.din("x_own", [NTOK, D])
        x_par = B.din("x_par", [NTOK, D])
        mk_i = B.din("mk", [128, 2, 128], BF16)
        a_norm = B.din("a_norm", [2, D])
        a_w_in = B.din("a_w_in", [2, D, 3 * E])
        a_ln_g = B.din("a_ln_g", [2, E])
        a_ln_b = B.din("a_ln_b", [2, E])
        a_w_s = B.din("a_w_s", [2, 8, 128, 128])
        a_b_s = B.din("a_b_s", [2, 8, 128])
        a_w_out = B.din("a_w_out", [2, E, D])
        kv_norm = B.din("kv_norm", [D])
        w_kv = B.din("w_kv", [D, 2 * E])
        b_norm = B.din("b_norm", [2, D])
        b_w_in = B.din("b_w_in", [2, D, 2 * E])
        b_w_out = B.din("b_w_out", [2, E, D])
        final_norm = B.din("final_norm", [D])
        y_o = B.dout("y", [NTOK, D])
        if DEBUG:
            kT_d = B.dout("kT_d", [2, 16, 128, NTOK], BF16)
            v_d = B.dout("v_d", [2, NTOK, E], BF16)
        else:
            kT_d = nc.dram_tensor("kT_d", [2, 16, 128, NTOK], BF16).ap()
            v_d = nc.dram_tensor("v_d", [2, NTOK, E], BF16).ap()

        B.wc = nc.dram_tensor("wc", [64, 128, 4096], BF16).ap()
        B.common_alloc()
        xres, hT, ident = B.xres, B.hT, B.ident
        SMALL = [0, 1, 2, 3]
        SQ = [4, 5, 6, 7]

        S.add("sp", lambda e: e.dma_start(out=xres[:], in_=x_own.rearrange("(t p) d -> p t d", p=128)),
              writes=[("x", t) for t in range(NB)], dma="xld")

        B.fence("setup")
        wsT = B.carve([128, 2, 8, 128], BF16)
        bsr = B.carve([33, 2, 1024], BF16, parts=33)
        ones33 = B.carve([33, 128], BF16, parts=33)
        keep = B.uoff
        wsf = B.carve([128, 8, 128], F32)
        wsb = B.carve([128, 8, 128], BF16)
        bsf = B.carve([33, 1024], F32, parts=33)
        bshi = B.carve([33, 1024], BF16, parts=33)
        bsd = B.carve([33, 1024], F32, parts=33)
        S.add("dve", lambda e: e.memset(ones33, 1.0), writes=["ones33"])
        for l in range(2):
            S.add("sp", lambda e, l=l: e.dma_start(out=wsf, in_=a_w_s[l].rearrange("g t s -> t g s")), writes=["wsf"], dma="wsf")
            S.add("dve", lambda e: e.memset(wsf[0:64, :, 64:128], 0.0), reads=["wsf"], writes=["wsf"])
            S.add("dve", lambda e: e.tensor_copy(out=wsb, in_=wsf), reads=["wsf"], writes=["wsb"])
            b = B.next_bank(SMALL)
            pv = B.bank(b).bitcast(BF16).rearrange("p (k n) -> p k n", k=8)
            for g in range(8):
                S.add("pe", lambda e, g=g, pv=pv: e.transpose(out=pv[:, g, :], in_=wsb[:, g, :], identity=ident[:]),
                      reads=["wsb", "ident"], writes=[B.bankres(b)])
            S.add("dve", lambda e, pv=pv, l=l: e.tensor_copy(out=wsT[:, l, :, :], in_=pv), reads=[B.bankres(b)], writes=["wsT"])
            S.add("dve", lambda e: e.memset(bsf, 0.0), writes=["bsf"])
            bsrc = a_b_s[l:l + 1].rearrange("o g t -> o (g t)")
            S.add("sp", lambda e, bsrc=bsrc: e.dma_start(out=bsf[0:1, :], in_=bsrc), reads=["bsf"], writes=["bsf"], dma="bsf")
            S.add("sp", lambda e, bsrc=bsrc: e.dma_start(out=bsf[32:33, :], in_=bsrc), reads=["bsf"], writes=["bsf"], dma="bsf")
            S.add("dve", lambda e: e.tensor_copy(out=bshi, in_=bsf), reads=["bsf"], writes=["bshi"])
            S.add("dve", lambda e: e.tensor_tensor(out=bsd, in0=bsf, in1=bshi, op=ALU.subtract), reads=["bsf", "bshi"], writes=["bsd"])
            S.add("dve", lambda e, l=l: e.tensor_copy(out=bsr[:, l, :], in_=bshi), reads=["bshi"], writes=["bsr"])
            S.add("dve", lambda e, l=l: e.tensor_copy(out=bsr[32:33, l, :], in_=bsd[32:33, :]), reads=["bsd", "bsr"], writes=["bsr"])

        B.fence("A")
        B.uoff = keep
        xpar = B.carve([128, 4, D], F32)
        vtok = B.carve([128, 2, E], F32)
        vln = B.carve([128, E], BF16)
        guz = B.carve([128, 16, 512], BF16)
        szt = [B.carve([128, 512], BF16) for _ in range(2)]
        lng = B.carve([128, E], F32)
        lnb = B.carve([128, E], F32)
        bst = B.carve([128, 2, 4, 6], F32)
        mv = B.carve([128, 2, 2], F32)
        lrs = B.carve([128, 4], F32)
        var4 = B.carve([128, 4], F32)
        vst = vtok.rearrange("p a b -> p (a b)").bitcast(BF16)

        for srcx in range(2):
            for tg in range(4):
                if srcx == 0:
                    xv = xres[:, 4 * tg:4 * tg + 4, :]
                    xn = [("x", 4 * tg + j) for j in range(4)]
                else:
                    xv = xpar
                    xn = [("xp", j) for j in range(4)]
                    S.add("sp", lambda e, tg=tg: e.dma_start(
                        out=xpar, in_=x_par[tg * 512:(tg + 1) * 512, :].rearrange("(t p) d -> p t d", p=128)),
                        writes=xn, dma="xpld")
                for l in range(2):
                    gt, gres = B.load_gain(a_norm[l])
                    S.add("sp", lambda e, l=l: e.dma_start(out=lng, in_=a_ln_g[l].partition_broadcast(128)), writes=["lng"], dma="lng")
                    S.add("sp", lambda e, l=l: e.dma_start(out=lnb, in_=a_ln_b[l].partition_broadcast(128)), writes=["lnb"], dma="lnb")
                    B.norm_group(xv, xn, gt, gres, SMALL, "act")
                    for part, col0 in (("u", 0), ("z", 2 * E)):
                        for cg in range(4):
                            c0 = col0 + cg * 512
                            si, wv = B.load_slot(("a", l, part, cg), a_w_in[l][:, c0:c0 + 512].rearrange("(k p) n -> p k n", p=128), 8)
                            for cc in range(4):
                                c = 4 * cg + cc
                                b = B.next_bank(SMALL)
                                pb = B.bank(b)
                                for k in range(8):
                                    S.add("pe", lambda e, k=k, wv=wv, cc=cc, pb=pb: e.matmul(
                                        pb, lhsT=wv[:, k, cc * 128:(cc + 1) * 128], rhs=hT[:, k, :], start=(k == 0), stop=(k == 7)),
                                        reads=[("w", si)] + HT_ALL, writes=[B.bankres(b)])
                                if part == "u":
                                    S.add("act", lambda e, c=c, pb=pb: e.activation(out=guz[:, c, :], in_=pb, func=AF.Gelu_apprx_tanh),
                                          reads=[B.bankres(b)], writes=[("guz", c)])
                                else:
                                    sz = szt[c % 2]
                                    S.add("act", lambda e, sz=sz, pb=pb: e.activation(out=sz, in_=pb, func=AF.Silu),
                                          reads=[B.bankres(b)], writes=[("szt", c % 2)])
                                    S.add("dve", lambda e, c=c, sz=sz: e.tensor_tensor(out=guz[:, c, :], in0=guz[:, c, :], in1=sz,
                                                                                       op=ALU.mult),
                                          reads=[("guz", c), ("szt", c % 2)], writes=[("guz", c)])
                    for hf in range(2):
                        for cg in range(4):
                            c0 = E + cg * 512
                            si, wv = B.load_slot(("a", l, "v", cg), a_w_in[l][:, c0:c0 + 512].rearrange("(k p) n -> p k n", p=128), 8)
                            for jj in range(2):
                                j = 2 * hf + jj
                                b = B.next_bank(SMALL)
                                pb = B.bank(b)
                                for k in range(8):
                                    S.add("pe", lambda e, k=k, wv=wv, j=j, pb=pb: e.matmul(
                                        pb, lhsT=hT[:, k, j * 128:(j + 1) * 128], rhs=wv[:, k, :], start=(k == 0), stop=(k == 7)),
                                        reads=[("w", si), ("hT", j)], writes=[B.bankres(b)])
                                S.add("act", lambda e, jj=jj, cg=cg, pb=pb: e.activation(
                                    out=vtok[:, jj, cg * 512:(cg + 1) * 512], in_=pb, func=AF.Gelu_apprx_tanh),
                                    reads=[B.bankres(b)], writes=[("vtok", jj, cg)])
                        for jj in range(2):
                            for cg in range(4):
                                S.add("dve", lambda e, jj=jj, cg=cg: e.bn_stats(out=bst[:, jj, cg, :], in_=vtok[:, jj, cg * 512:(cg + 1) * 512]),
                                      reads=[("vtok", jj, cg)], writes=[("bst", jj)])
                            S.add("dve", lambda e, jj=jj: e.bn_aggr(out=mv[:, jj, :], in_=bst[:, jj, :, :].rearrange("p a b -> p (a b)")),
                                  reads=[("bst", jj)], writes=[("mv", jj)])
                            S.add("dve", lambda e, jj=jj: e.tensor_copy(out=var4[:, jj:jj + 1], in_=mv[:, jj, 1:2]), reads=[("mv", jj)],
                                  writes=[("var4", jj)])
                        B.rstd4(var4, lrs, 2, 1.0, [("var4", jj) for jj in range(2)], ["lrs"])
                        for jj in range(2):
                            j = 2 * hf + jj
                            vres = [("vtok", jj, cg) for cg in range(4)]
                            S.add("dve", lambda e, jj=jj: e.scalar_tensor_tensor(out=vtok[:, jj, :], in0=vtok[:, jj, :], scalar=mv[:, jj, 0:1],
                                                                               in1=lng, op0=ALU.subtract, op1=ALU.mult),
                                  reads=vres + [("mv", jj), "lng"], writes=vres)
                            S.add("dve", lambda e, jj=jj: e.scalar_tensor_tensor(out=vln, in0=vtok[:, jj, :], scalar=lrs[:, jj:jj + 1],
                                                                               in1=lnb, op0=ALU.mult, op1=ALU.add),
                                  reads=vres + ["lrs", "lnb"], writes=["vln"])
                            for qq in range(4):
                                b = B.next_bank(SQ)
                                pb = B.bank(b)
                                for cc in range(4):
                                    c = 4 * qq + cc
                                    g = c // 2
                                    S.add("pe", lambda e, c=c, cc=cc, g=g, pb=pb, l=l: e.matmul(
                                        pb[:, cc * 128:(cc + 1) * 128], lhsT=vln[:, c * 128:(c + 1) * 128], rhs=wsT[:, l, g, :],
                                        start=True, stop=False), reads=["vln", "wsT"], writes=[B.bankres(b)])
                                    S.add("pe", lambda e, cc=cc, g=g, pb=pb, l=l: e.matmul(
                                        pb[:, cc * 128:(cc + 1) * 128], lhsT=ones33, rhs=bsr[:, l, g * 128:(g + 1) * 128],
                                        start=False, stop=True), reads=["ones33", "bsr"], writes=[B.bankres(b)])
                                gview = guz[:, 4 * qq:4 * qq + 4, j * 128:(j + 1) * 128]
                                S.add("dve", lambda e, pb=pb, gview=gview: e.tensor_tensor(
                                    out=gview, in0=pb.rearrange("p (a b) -> p a b", a=4), in1=gview, op=ALU.mult),
                                    reads=[B.bankres(b)] + [("guz", 4 * qq + cc) for cc in range(4)],
                                    writes=[("guz", 4 * qq + cc) for cc in range(4)])
                    wo = []
                    for q in range(4):
                        wo.append(B.load_slot(("a", l, "wo", q), a_w_out[l][q * 512:(q + 1) * 512, :].rearrange("(k p) n -> p k n", p=128), 4))
                    for j in range(4):
                        for half in range(2):
                            b = B.next_bank(SMALL)
                            pb = B.bank(b)
                            for c in range(16):
                                si, wv = wo[c // 4]
                                S.add("pe", lambda e, c=c, wv=wv, half=half, pb=pb, j=j: e.matmul(
                                    pb, lhsT=guz[:, c, j * 128:(j + 1) * 128], rhs=wv[:, c % 4, half * 512:(half + 1) * 512],
                                    start=(c == 0), stop=(c == 15)), reads=[("guz", c), ("w", si)], writes=[B.bankres(b)])
                            xw = xv[:, j, half * 512:(half + 1) * 512]
                            S.add("dve", lambda e, xw=xw, pb=pb: e.tensor_tensor(out=xw, in0=xw, in1=pb, op=ALU.add),
                                  reads=[xn[j], B.bankres(b)], writes=[xn[j]])
                gt, gres = B.load_gain(kv_norm)
                B.norm_group(xv, xn, gt, gres, SMALL, "act")
                for cg in range(4):
                    si, wv = B.load_slot(("kv", "k", cg), w_kv[:, cg * 512:(cg + 1) * 512].rearrange("(k p) n -> p k n", p=128), 8)
                    kres = [("guz", 4 * (cg % 2) + hh) for hh in range(4)]
                    for hh in range(4):
                        b = B.next_bank(SMALL)
                        pb = B.bank(b)
                        for k in range(8):
                            S.add("pe", lambda e, k=k, wv=wv, hh=hh, pb=pb: e.matmul(
                                pb, lhsT=wv[:, k, hh * 128:(hh + 1) * 128], rhs=hT[:, k, :], start=(k == 0), stop=(k == 7)),
                                reads=[("w", si)] + HT_ALL, writes=[B.bankres(b)])
                        dst = guz[:, 4 * (cg % 2) + hh, :]
                        S.add("act", lambda e, dst=dst, pb=pb: e.copy(out=dst, in_=pb), reads=[B.bankres(b)], writes=[kres[hh]])
                    src = guz[:, 4 * (cg % 2):4 * (cg % 2) + 4, :]
                    dstd = kT_d[srcx, 4 * cg:4 * cg + 4, :, tg * 512:(tg + 1) * 512].rearrange("h d t -> d h t")
                    S.add("sp", lambda e, src=src, dstd=dstd: e.dma_start(out=dstd, in_=src), reads=kres, dma="kst%d" % (cg % 2))
                for cg in range(4):
                    si, wv = B.load_slot(("kv", "v", cg), w_kv[:, E + cg * 512:E + (cg + 1) * 512].rearrange("(k p) n -> p k n", p=128), 8)
                    for j in range(4):
                        b = B.next_bank(SMALL)
                        pb = B.bank(b)
                        for k in range(8):
                            S.add("pe", lambda e, k=k, wv=wv, j=j, pb=pb: e.matmul(
                                pb, lhsT=hT[:, k, j * 128:(j + 1) * 128], rhs=wv[:, k, :], start=(k == 0), stop=(k == 7)),
                                reads=[("w", si), ("hT", j)], writes=[B.bankres(b)])
                        dst = vst[:, j * 2048 + cg * 512: j * 2048 + (cg + 1) * 512]
                        S.add("dve", lambda e, dst=dst, pb=pb: e.tensor_copy(out=dst, in_=pb), reads=[B.bankres(b)],
                              writes=[("vtok", j // 2, 2 * (j % 2) + cg // 2)])
                for j in range(4):
                    t = 4 * tg + j
                    src = vst[:, j * 2048:(j + 1) * 2048]
                    S.add("sp", lambda e, src=src, t=t, srcx=srcx: e.dma_start(out=v_d[srcx, t * 128:(t + 1) * 128, :], in_=src),
                          reads=[("vtok", j // 2, 2 * (j % 2) + h2) for h2 in range(2)], dma="vst%d" % j)

        B.fence("B")
        qT = B.carve([128, 16, 512], BF16)
        yT = B.carve([128, 16, 512], BF16)
        KT = [B.carve([128, 4096], BF16) for _ in range(2)]
        VT = [B.carve([128, 32, 128], BF16) for _ in range(2)]
        Eb = [B.carve([128, 512], F32) for _ in range(2)]
        SPb = [B.carve([128, 512], BF16) for _ in range(2)]
        ATb = [B.carve([128, 512], BF16) for _ in range(2)]
        Xb = [B.carve([128, 512], F32) for _ in range(2)]
        Csb = [B.carve([128, 512], F32) for _ in range(2)]
        mk = B.carve([128, 2, 128], BF16)
        ntri = B.carve([128, 128], BF16)
        ntrif = B.carve([128, 128], F32)
        onesm = B.carve([128, 128], BF16)
        outb = B.carve([128, D], F32)

        S.add("pool", lambda e: e.memset(ntrif, -1.0), writes=["ntrif"])
        S.add("pool", lambda e: e.affine_select(out=ntrif, in_=ntrif, pattern=[[-1, 128]], compare_op=ALU.is_ge,
                                               fill=0.0, base=0, channel_multiplier=1), reads=["ntrif"], writes=["ntrif"])
        S.add("dve", lambda e: e.tensor_copy(out=ntri, in_=ntrif), reads=["ntrif"], writes=["ntri"])
        S.add("dve", lambda e: e.memset(onesm, 1.0), writes=["onesm"])
        S.add("sp", lambda e: e.dma_start(out=mk, in_=mk_i), writes=["mk"], dma="mk")

        ABANK = [0, 1, 2]
        OBANK = [3, 4]
        RBANK = [5, 6]
        PROJ = [3, 4, 5, 6, 7]
        head_ctr = 0
        gstep = 0

        def slot_col(k):
            return (2048 + (k // 2) * 128) if k % 2 == 0 else ((k - 1) // 2) * 128

        for l in range(2):
            for tg in range(4):
                xv = xres[:, 4 * tg:4 * tg + 4, :]
                xn = [("x", 4 * tg + j) for j in range(4)]
                gt, gres = B.load_gain(b_norm[l])
                B.norm_group(xv, xn, gt, gres, PROJ, "dve")
                nsl = 8 * tg + 8
                nloc = 4 * tg + 4
                nblk = [8 * tg + 2, 8 * tg + 4, 8 * tg + 6, 8 * tg + 8]
                for part, col0 in (("q", 0), ("z", E)):
                    for cg in range(4):
                        c0 = col0 + cg * 512
                        si, wv = B.load_slot(("b", l, part, cg), b_w_in[l][:, c0:c0 + 512].rearrange("(k p) n -> p k n", p=128), 8)
                        for cc in range(4):
                            c = 4 * cg + cc
                            b = B.next_bank(PROJ)
                            pb = B.bank(b)
                            for k in range(8):
                                S.add("pe", lambda e, k=k, wv=wv, cc=cc, pb=pb: e.matmul(
                                    pb, lhsT=wv[:, k, cc * 128:(cc + 1) * 128], rhs=hT[:, k, :], start=(k == 0), stop=(k == 7)),
                                    reads=[("w", si)] + HT_ALL, writes=[B.bankres(b)])
                            if part == "q":
                                S.add("dve", lambda e, c=c, pb=pb: e.tensor_scalar(out=qT[:, c, :], in0=pb, scalar1=QSCALE,
                                                                                   scalar2=None, op0=ALU.mult),
                                      reads=[B.bankres(b)], writes=[("qT", c)])
                            else:
                                S.add("act", lambda e, c=c, pb=pb: e.activation(out=yT[:, c, :], in_=pb, func=AF.Silu),
                                      reads=[B.bankres(b)], writes=[("yT", c)])
                def c0_of(k, nblk=nblk):
                    return 128 * sum(1 for n in nblk if n <= k)

                flat = []
                for hd in range(16):
                    kb = head_ctr % 2
                    head_ctr += 1
                    for k in range(nsl - 1, -1, -1):
                        flat.append(dict(hd=hd, kb=kb, k=k, c0=c0_of(k), first=(k == nsl - 1), last=(k == 0)))

                def kv_load(hd, kb, nloc=nloc):
                    Kt, Vt = KT[kb], VT[kb]
                    for sx in range(2):
                        S.add("sp", lambda e, Kt=Kt, hd=hd, nloc=nloc, sx=sx: e.dma_start(
                            out=Kt[:, sx * 2048:sx * 2048 + nloc * 128], in_=kT_d[sx, hd, :, 0:nloc * 128]),
                            writes=[("KT", kb)], dma="KT%d%d" % (kb, sx))
                        S.add("sp", lambda e, Vt=Vt, hd=hd, nloc=nloc, sx=sx: e.dma_start(
                            out=Vt[:, sx * 16:sx * 16 + nloc, :],
                            in_=v_d[sx, 0:nloc * 128, hd * 128:(hd + 1) * 128].rearrange("(k p) d -> p k d", p=128)),
                            writes=[("VT", kb)], dma="VT%d%d" % (kb, sx))
                for g, st in enumerate(flat):
                    st["g"] = gstep + g
                gstep += len(flat)

                def f_logits(st):
                    hd, kb, k, c0, g = st["hd"], st["kb"], st["k"], st["c0"], st["g"]
                    Kt = KT[kb]
                    if st["first"]:
                        Ob = B.bank(OBANK[kb])
                        Cs = Csb[kb]
                        S.add("dve", lambda e, Ob=Ob: e.memset(Ob, 0.0), writes=[B.bankres(OBANK[kb])])
                        S.add("dve", lambda e, Cs=Cs: e.memset(Cs, 0.0), writes=[("Csb", kb)])
                    ab = ABANK[g % 3]
                    Ab = B.bank(ab)
                    msk = [(i, 0) for i in range(4) if nblk[i] - 1 == k]
                    if k == 0:
                        msk += [(i, 1) for i in range(4)]
                    kc = slot_col(k)
                    S.add("pe", lambda e, Ab=Ab, kc=kc, c0=c0, last=(not msk), Kt=Kt, hd=hd: e.matmul(
                        Ab[:, c0:512], lhsT=Kt[:, kc:kc + 128], rhs=qT[:, hd, c0:512], start=True, stop=last),
                        reads=[("KT", kb), ("qT", hd)], writes=[B.bankres(ab)])
                    for n_, (i, kind) in enumerate(msk):
                        S.add("pe", lambda e, Ab=Ab, i=i, kind=kind, last=(n_ == len(msk) - 1): e.matmul(
                            Ab[:, i * 128:(i + 1) * 128], lhsT=ident[:], rhs=mk[:, kind, :], start=False, stop=last),
                            reads=["ident", "mk"], writes=[B.bankres(ab)])

                def f_exp(st):
                    c0, g = st["c0"], st["g"]
                    ab = ABANK[g % 3]
                    Ab = B.bank(ab)
                    Et = Eb[g % 2]
                    S.add("act", lambda e, Ab=Ab, Et=Et, c0=c0: e.activation(out=Et[:, c0:512], in_=Ab[:, c0:512], func=AF.Exp),
                          reads=[B.bankres(ab)], writes=[("E", g % 2)])

                def f_sp(st):
                    c0, g, kb = st["c0"], st["g"], st["kb"]
                    ab = ABANK[g % 3]
                    Ab = B.bank(ab)
                    Et, SPt, Xt = Eb[g % 2], SPb[g % 2], Xb[g % 2]
                    Cs = Csb[kb]
                    S.add("act", lambda e, Et=Et, SPt=SPt, c0=c0: e.activation(out=SPt[:, c0:512], in_=Et[:, c0:512], func=AF.Ln,
                                                                                bias=1.0),
                          reads=[("E", g % 2)], writes=[("SP", g % 2)])
                    S.add("pe", lambda e, Ab=Ab, SPt=SPt, c0=c0: e.matmul(
                        Ab[:, c0:512], lhsT=ntri, rhs=SPt[:, c0:512], start=False, stop=True, skip_group_check=True),
                        reads=["ntri", ("SP", g % 2)], writes=[B.bankres(ab)])
                    if not st["last"]:
                        tb = RBANK[g % 2]
                        Tb = B.bank(tb)
                        S.add("pe", lambda e, SPt=SPt, c0=c0, Tb=Tb: e.matmul(
                            Tb[:, c0:512], lhsT=onesm, rhs=SPt[:, c0:512], start=True, stop=True),
                            reads=["onesm", ("SP", g % 2)], writes=[B.bankres(tb)])
                    S.add("dve", lambda e, Ab=Ab, Xt=Xt, c0=c0, Cs=Cs: e.tensor_tensor(
                        out=Xt[:, c0:512], in0=Ab[:, c0:512], in1=Cs[:, c0:512], op=ALU.subtract),
                        reads=[B.bankres(ab), ("Csb", kb)], writes=[("X", g % 2)])
                    if not st["last"]:
                        S.add("dve", lambda e, c0=c0, Tb=Tb, Cs=Cs: e.tensor_tensor(
                            out=Cs[:, c0:512], in0=Tb[:, c0:512], in1=Cs[:, c0:512], op=ALU.add),
                            reads=[B.bankres(tb), ("Csb", kb)], writes=[("Csb", kb)])

                def f_att(st):
                    hd, kb, k, c0, g = st["hd"], st["kb"], st["k"], st["c0"], st["g"]
                    Xt, At = Xb[g % 2], ATb[g % 2]
                    Ob = B.bank(OBANK[kb])
                    Vt = VT[kb]
                    vb = slot_col(k) // 128
                    S.add("act", lambda e, Xt=Xt, At=At, c0=c0: e.activation(out=At[:, c0:512], in_=Xt[:, c0:512], func=AF.Exp),
                          reads=[("X", g % 2)], writes=[("AT", g % 2)])
                    S.add("pe", lambda e, At=At, vb=vb, c0=c0, Ob=Ob, Vt=Vt: e.matmul(
                        Ob[:, c0:512], lhsT=Vt[:, vb, :], rhs=At[:, c0:512], start=False, stop=True, skip_group_check=True),
                        reads=[("VT", kb), ("AT", g % 2)], writes=[B.bankres(OBANK[kb])])
                    if st["last"]:
                        S.add("dve", lambda e, Ob=Ob, hd=hd: e.tensor_tensor(out=yT[:, hd, :], in0=Ob, in1=yT[:, hd, :], op=ALU.mult),
                              reads=[B.bankres(OBANK[kb]), ("yT", hd)], writes=[("yT", hd)])

                nf = len(flat)
                kb_of = {st["hd"]: st["kb"] for st in flat}
                kv_load(0, kb_of[0])
                kv_load(1, kb_of[1])
                f_logits(flat[0])
                if nf > 1:
                    f_logits(flat[1])
                f_exp(flat[0])
                for i in range(nf + 1):
                    if i + 2 < nf:
                        f_logits(flat[i + 2])
                    if i + 1 < nf:
                        f_exp(flat[i + 1])
                    if i < nf:
                        f_sp(flat[i])
                    if i >= 1:
                        f_att(flat[i - 1])
                        if i < nf and flat[i]["first"] and 1 <= flat[i]["hd"] <= 14:
                            kv_load(flat[i]["hd"] + 1, kb_of[flat[i]["hd"] + 1])
                wo = []
                for q in range(4):
                    wo.append(B.load_slot(("b", l, "wo", q), b_w_out[l][q * 512:(q + 1) * 512, :].rearrange("(k p) n -> p k n", p=128), 4))
                for j in range(4):
                    t = 4 * tg + j
                    for half in range(2):
                        b = B.next_bank(PROJ)
                        pb = B.bank(b)
                        for c in range(16):
                            si, wv = wo[c // 4]
                            S.add("pe", lambda e, c=c, wv=wv, half=half, pb=pb, j=j: e.matmul(
                                pb, lhsT=yT[:, c, j * 128:(j + 1) * 128], rhs=wv[:, c % 4, half * 512:(half + 1) * 512],
                                start=(c == 0), stop=(c == 15)), reads=[("yT", c), ("w", si)], writes=[B.bankres(b)])
                        xw = xres[:, t, half * 512:(half + 1) * 512]
                        S.add("dve", lambda e, xw=xw, pb=pb: e.tensor_tensor(out=xw, in0=xw, in1=pb, op=ALU.add),
                              reads=[("x", t), B.bankres(b)], writes=[("x", t)])

        gt, gres = B.load_gain(final_norm)
        rs = B.rs
        for tg in range(4):
            xv = xres[:, 4 * tg:4 * tg + 4, :]
            xn = [("x", 4 * tg + j) for j in range(4)]
            B.sumsq4(xv, xn)
            for j in range(4):
                t = 4 * tg + j
                S.add("dve", lambda e, t=t, j=j, gt=gt: e.scalar_tensor_tensor(
                    out=outb, in0=xres[:, t, :], scalar=rs[:, j:j + 1], in1=gt[:], op0=ALU.mult, op1=ALU.mult),
                    reads=[("x", t), "rs", gres], writes=["outb"])
                S.add("sp", lambda e, t=t: e.dma_start(out=y_o[t * 128:(t + 1) * 128, :], in_=outb),
                      reads=["outb"], dma="ost")
        B.finish()
    return nc


def _mask_tiles(j):
    s = np.arange(128)[:, None]
    t = np.arange(128)[None, :]
    tri = np.where(s >= t, NEG, 0.0).astype(np.float32)
    p0 = np.full((128, 128), NEG if j == 0 else 0.0, np.float32)
    return np.ascontiguousarray(np.stack([tri, p0], axis=1)).astype(ml_dtypes.bfloat16)


_NC_CACHE = {}


def kernel(x, a_norm, a_w_in, a_ln_g, a_ln_b, a_w_s, a_b_s, a_w_out, kv_norm, w_kv,
           b_norm, b_w_in, b_w_out, final_norm):
    f = lambda a: np.ascontiguousarray(np.asarray(a, dtype=np.float32))
    x = f(x)
    Bn, Sq, _ = x.shape
    xb = x.reshape(Bn, Sq // 128, 128, D)
    wts = {"a_norm": f(a_norm), "a_w_in": f(a_w_in), "a_ln_g": f(a_ln_g), "a_ln_b": f(a_ln_b), "a_w_s": f(a_w_s),
           "a_b_s": f(a_b_s), "a_w_out": f(a_w_out), "kv_norm": f(kv_norm), "w_kv": f(w_kv), "b_norm": f(b_norm),
           "b_w_in": f(b_w_in), "b_w_out": f(b_w_out), "final_norm": f(final_norm)}
    cores = [(b, j) for b in range(Bn) for j in range(2)]
    in_maps = []
    for (b, j) in cores:
        own = np.ascontiguousarray(xb[b, j::2].reshape(NTOK, D))
        if j == 1:
            par = np.ascontiguousarray(xb[b, 0::2].reshape(NTOK, D))
        else:
            par = np.zeros((NB, 128, D), np.float32)
            par[1:] = xb[b, 1::2][:NB - 1]
            par = par.reshape(NTOK, D)
        m = {"x_own": own, "x_par": par, "mk": _mask_tiles(j)}
        m.update(wts)
        in_maps.append(m)
    if "f" not in _NC_CACHE:
        _NC_CACHE["f"] = build_fused()
    res = run_bass_kernel_spmd(_NC_CACHE["f"], in_maps, core_ids=list(range(8))).results
    if DEBUG:
        LAST["res"] = res
    out = np.zeros((Bn, Sq // 128, 128, D), dtype=np.float32)
    for ci, (b, j) in enumerate(cores):
        out[b, j::2] = np.asarray(res[ci]["y"], dtype=np.float32).reshape(NB, 128, D)
    return out.reshape(Bn, Sq, D)
```
